# Optimizing a Trainium2 kernel written in Bass

```python
import jax, jax.numpy as jnp
from jax import lax
import numpy as np

D_MODEL = 1024
BATCH = 8
SEQ = 8192
DEPTH = 4

CHUNK = 64
N_META = 16
HEAD_DIM = 64
D_RW = D_MODEL // 2
H_RW = D_RW // HEAD_DIM
D_FX = D_MODEL // 2
H_FX = D_FX // HEAD_DIM
LORA_W = 64
LORA_A = 64
LORA_G = 128
D_FF = 4 * D_MODEL
QBLK = 128
C_RW = 3 * D_RW + LORA_W + LORA_A + LORA_G
C_FX = 3 * D_FX + H_FX
C_GATE = 2 * D_MODEL
C_IN = C_RW + C_FX + C_GATE
NORM_EPS = 1e-6
GN_EPS = 64e-5
DECAY_OFFSET = 0.5

kernel_name = "rwkv7_fox_gated_hybrid_trunk"


def rmsnorm(x, g):
    xf = x.astype(jnp.float32)
    y = xf * lax.rsqrt(jnp.mean(xf * xf, axis=-1, keepdims=True) + NORM_EPS)
    return (y * g.astype(jnp.float32)).astype(x.dtype)


def token_shift(f):
    return jnp.pad(f, ((0, 0), (1, 0), (0, 0)))[:, :-1]


def wkv7_scan(r, w, k, v, a, b):
    Bn, Ln, Hn, Nn = r.shape
    s0 = jnp.zeros((Bn, Hn, Nn, Nn), jnp.float32)
    xs = tuple(jnp.moveaxis(t, 1, 0) for t in (r, w, k, v, a, b))

    def step(S, inp):
        rt, wt, kt, vt, at, bt = inp
        sa = jnp.einsum('bhij,bhj->bhi', S, at)
        S = S * wt[:, :, None, :] + sa[..., None] * bt[:, :, None, :] + vt[..., None] * kt[:, :, None, :]
        yt = jnp.einsum('bhij,bhj->bhi', S, rt)
        return S, yt

    _, y = lax.scan(step, s0, xs)
    return jnp.moveaxis(y, 0, 1)


def rwkv7_branch(feat, mu, w0, w_lora_up, a0, a_lora_up, g_lora_up, k_k, k_a, r_k, lnx_g, lnx_b):
    Bn, Ln, _ = feat.shape
    f = feat.astype(jnp.float32)
    f = f + mu * (token_shift(f) - f)
    r, k, v, wd, ad, gd = jnp.split(
        f, [D_RW, 2 * D_RW, 3 * D_RW, 3 * D_RW + LORA_W, 3 * D_RW + LORA_W + LORA_A], axis=-1)
    w_log = -jax.nn.softplus(-(w0 + jnp.tanh(wd) @ w_lora_up)) - DECAY_OFFSET
    decay = jnp.exp(-jnp.exp(w_log))
    a = jax.nn.sigmoid(a0 + ad @ a_lora_up)
    g = jax.nn.sigmoid(gd) @ g_lora_up
    hv = lambda t: t.reshape(Bn, Ln, H_RW, HEAD_DIM)
    kk = hv(k * k_k)
    kk = kk / jnp.maximum(jnp.sqrt(jnp.sum(kk * kk, axis=-1, keepdims=True)), 1e-12)
    k = k * (1.0 + (a - 1.0) * k_a)
    a_h = hv(a)
    y = wkv7_scan(hv(r), hv(decay), hv(k), hv(v), -kk, kk * a_h)
    mean = jnp.mean(y, axis=-1, keepdims=True)
    var = jnp.mean(jnp.square(y - mean), axis=-1, keepdims=True)
    y = (y - mean) * lax.rsqrt(var + GN_EPS)
    y = y.reshape(Bn, Ln, D_RW) * lnx_g + lnx_b
    bonus = jnp.sum(hv(r) * hv(k) * r_k, axis=-1, keepdims=True) * hv(v)
    return (y + bonus.reshape(Bn, Ln, D_RW)) * g


def head_rmsnorm(t, g):
    tf = t.astype(jnp.float32)
    return tf * lax.rsqrt(jnp.mean(tf * tf, axis=-1, keepdims=True) + NORM_EPS) * g


def fox_branch(feat, b_f, q_gain, k_gain):
    Bn, Ln, _ = feat.shape
    q, k, v, fl = jnp.split(feat, [D_FX, 2 * D_FX, 3 * D_FX], axis=-1)
    q = head_rmsnorm(q.reshape(Bn, Ln, H_FX, HEAD_DIM), q_gain)
    k = head_rmsnorm(k.reshape(Bn, Ln, H_FX, HEAD_DIM), k_gain)
    v = v.reshape(Bn, Ln, H_FX, HEAD_DIM)
    log_f = jax.nn.log_sigmoid(fl.astype(jnp.float32) + b_f)
    Lp = -(-Ln // QBLK) * QBLK
    pad = Lp - Ln
    q = jnp.pad(q, ((0, 0), (0, pad), (0, 0), (0, 0)))
    k = jnp.pad(k, ((0, 0), (0, pad), (0, 0), (0, 0)))
    v = jnp.pad(v, ((0, 0), (0, pad), (0, 0), (0, 0)))
    F = jnp.cumsum(jnp.pad(log_f, ((0, 0), (0, pad), (0, 0))), axis=1)
    F = jnp.transpose(F, (0, 2, 1))
    kpos = jnp.arange(Lp)
    scale = HEAD_DIM ** -0.5

    def block(i):
        start = i * QBLK
        qb = lax.dynamic_slice_in_dim(q, start, QBLK, axis=1)
        Fq = lax.dynamic_slice_in_dim(F, start, QBLK, axis=2)
        logits = jnp.einsum('bqhd,bkhd->bhqk', qb, k).astype(jnp.float32) * scale
        logits = logits + (Fq[..., None] - F[:, :, None, :])
        qpos = start + jnp.arange(QBLK)
        mask = kpos[None, :] <= qpos[:, None]
        p = jax.nn.softmax(jnp.where(mask, logits, -jnp.inf), axis=-1)
        return jnp.einsum('bhqk,bkhd->bqhd', p.astype(v.dtype), v)

    out = lax.map(block, jnp.arange(Lp // QBLK))
    out = jnp.moveaxis(out, 0, 1).reshape(Bn, Lp, D_FX)
    return out[:, :Ln]


def setup_inputs(seed: int = 0) -> dict:
    key = jax.random.key(seed)
    ks = jax.random.split(key, 26)
    nrm = lambda k, shape, s: jax.random.normal(k, shape, jnp.float32) * s
    L = DEPTH
    return {
        "x": nrm(ks[0], (BATCH, SEQ, D_MODEL), 1.0),
        "meta": nrm(ks[1], (N_META, D_MODEL), 1.0),
        "norm_mix": 1.0 + nrm(ks[2], (L, D_MODEL), 0.02),
        "w_in": nrm(ks[3], (L, D_MODEL, C_IN), D_MODEL ** -0.5),
        "b_gate": nrm(ks[4], (L, C_GATE), 0.1),
        "b_f": 1.0 + nrm(ks[5], (L, H_FX), 0.5),
        "tm_mu": jax.random.uniform(ks[6], (L, C_RW), jnp.float32),
        "w0": jax.random.uniform(ks[7], (L, D_RW), jnp.float32, -4.0, 1.0),
        "w_lora_up": nrm(ks[8], (L, LORA_W, D_RW), LORA_W ** -0.5),
        "a0": nrm(ks[9], (L, D_RW), 0.1),
        "a_lora_up": nrm(ks[10], (L, LORA_A, D_RW), LORA_A ** -0.5),
        "g_lora_up": nrm(ks[11], (L, LORA_G, D_RW), LORA_G ** -0.5),
        "k_k": 0.85 + nrm(ks[12], (L, D_RW), 0.05),
        "k_a": 1.0 + nrm(ks[13], (L, D_RW), 0.05),
        "r_k": nrm(ks[14], (L, H_RW, HEAD_DIM), 0.1),
        "lnx_g": 1.0 + nrm(ks[15], (L, D_RW), 0.02),
        "lnx_b": nrm(ks[16], (L, D_RW), 0.02),
        "q_gain": 1.0 + nrm(ks[17], (L, HEAD_DIM), 0.02),
        "k_gain": 1.0 + nrm(ks[18], (L, HEAD_DIM), 0.02),
        "w_out_rw": nrm(ks[19], (L, D_RW, D_MODEL), D_RW ** -0.5),
        "w_out_fx": nrm(ks[20], (L, D_FX, D_MODEL), D_FX ** -0.5),
        "w_o": nrm(ks[21], (L, D_MODEL, D_MODEL), D_MODEL ** -0.5),
        "norm_mlp": 1.0 + nrm(ks[22], (L, D_MODEL), 0.02),
        "w_up": nrm(ks[23], (L, D_MODEL, D_FF), D_MODEL ** -0.5),
        "w_down": nrm(ks[24], (L, D_FF, D_MODEL), D_FF ** -0.5),
    }


def reference(x, meta, norm_mix, w_in, b_gate, b_f, tm_mu, w0, w_lora_up, a0, a_lora_up,
              g_lora_up, k_k, k_a, r_k, lnx_g, lnx_b, q_gain, k_gain, w_out_rw, w_out_fx,
              w_o, norm_mlp, w_up, w_down):
    Bn = x.shape[0]
    meta_b = jnp.broadcast_to(meta.astype(x.dtype)[None], (Bn, N_META, D_MODEL))
    h = jnp.concatenate([meta_b, x], axis=1)
    for l in range(DEPTH):
        u = rmsnorm(h, norm_mix[l])
        proj = u @ w_in[l]
        feat_rw = proj[..., :C_RW]
        feat_fx = proj[..., C_RW:C_RW + C_FX]
        gates = jax.nn.sigmoid((proj[..., C_RW + C_FX:] + b_gate[l]).astype(jnp.float32))
        y_rw = rwkv7_branch(feat_rw, tm_mu[l], w0[l], w_lora_up[l], a0[l], a_lora_up[l],
                            g_lora_up[l], k_k[l], k_a[l], r_k[l], lnx_g[l], lnx_b[l])
        y_fx = fox_branch(feat_fx, b_f[l], q_gain[l], k_gain[l])
        merged = gates[..., :D_MODEL] * (y_rw @ w_out_rw[l]) + gates[..., D_MODEL:] * (y_fx @ w_out_fx[l])
        h = h + (merged @ w_o[l]).astype(h.dtype)
        z = rmsnorm(h, norm_mlp[l])
        h = h + (jnp.square(jax.nn.relu(z @ w_up[l])) @ w_down[l]).astype(h.dtype)
    return h[:, N_META:]
```

```python
from contextlib import ExitStack
import numpy as np
import concourse.bass as bass
import concourse.mybir as mybir
from concourse.bass_utils import run_bass_kernel_spmd

F32 = mybir.dt.float32
BF16 = mybir.dt.bfloat16
AF = mybir.ActivationFunctionType
ALU = mybir.AluOpType

ENGS = ("pe", "act", "dve", "pool", "sp")

D = 1024
SEQ = 8192
NMETA = 16
DEPTH = 4
CIN = 5384
DFF = 4096
NORM_EPS = 1e-6
GN_EPS = 64e-5
LWS = 0.6065306597126334

C_NMIX, C_BG, C_MU, C_W0, C_A0, C_KK, C_KA, C_RK, C_LNG, C_LNB, C_QG, C_KG, C_NMLP, C_BF = (
    0, 8, 24, 38, 42, 46, 50, 54, 58, 62, 66, 67, 68, 76)
NCOL = 80


class Sched:
    def __init__(self, nc):
        self.nc = nc
        self.prog = {e: [] for e in ENGS}
        self.res = {}
        self.dma_cnt = {}
        self.phase_slots = {}
        self.n_ops = {e: 0 for e in ENGS}

    def _collect(self, eng, reads, writes):
        deps = {}

        def add(tk, v):
            if deps.get(tk, -1) < v:
                deps[tk] = v

        for r in reads:
            st = self.res.get(r)
            if st:
                for tk, v in st[0].items():
                    add(tk, v)
        for w in writes:
            st = self.res.get(w)
            if st:
                for tk, v in st[0].items():
                    if tk == ("E", eng):
                        continue
                    add(tk, v)
                for tk, v in st[1].items():
                    if tk == ("E", eng):
                        continue
                    add(tk, v)
        return deps

    def _update(self, tok, val, reads, writes):
        for r in reads:
            st = self.res.setdefault(r, [{}, {}])
            st[1][tok] = val
        for w in writes:
            self.res[w] = [{tok: val}, {}]

    def op(self, eng, fn, reads=(), writes=()):
        deps = self._collect(eng, reads, writes)
        self.n_ops[eng] += 1
        o = self.n_ops[eng]
        self.prog[eng].append(("op", fn, deps, o))
        self._update(("E", eng), o, reads, writes)

    def dma(self, eng, fn, slot, reads=(), writes=()):
        deps = self._collect(eng, reads, writes)
        slot = self.phase_slots.setdefault(slot, len(self.phase_slots))
        c = self.dma_cnt.get(slot, 0) + 16
        self.dma_cnt[slot] = c
        self.prog[eng].append(("dma", fn, deps, slot))
        self._update(("D", slot), c, reads, writes)

    def barrier(self):
        toks = {}
        for e in ENGS:
            if self.n_ops[e] > 0:
                toks[("E", e)] = self.n_ops[e]
        for s, c in self.dma_cnt.items():
            toks[("D", s)] = c
        for e in ENGS:
            d = {k: v for k, v in toks.items() if k != ("E", e)}
            self.prog[e].append(("wait", None, d, None))
        self.res = {}
        self.phase_slots = {}

    def emit(self, stack):
        nc = self.nc
        needed = {e: set() for e in ENGS}
        for e in ENGS:
            for rec in self.prog[e]:
                for (kind, name), v in rec[2].items():
                    if kind == "E":
                        needed[name].add(v)
        EPOCH = getattr(self, "EPOCH", 30000)
        val_of = {}
        esem = {}
        for e in ENGS:
            for i, o in enumerate(sorted(needed[e])):
                ep = i // EPOCH
                val_of[(e, o)] = (ep, i % EPOCH + 1)
                if (e, ep) not in esem:
                    esem[(e, ep)] = stack.enter_context(nc.semaphore(name="es_%s%d" % (e, ep)))
        dsem = {}
        for i, s in enumerate(self.dma_cnt):
            dsem[s] = stack.enter_context(nc.semaphore(name="ds_%d" % i))
        self.stats = {e: [0, 0] for e in ENGS}
        engobj = {"pe": "tensor", "act": "scalar", "dve": "vector", "pool": "gpsimd", "sp": "sync"}
        block = stack.enter_context(nc.Block())
        for e in ENGS:
            prog = self.prog[e]
            if not prog:
                continue

            def body(engine, e=e, prog=prog):
                seen = {}
                for kind, fn, deps, aux in prog:
                    for (k, name), v in deps.items():
                        if k == "E":
                            ep, val = val_of[(name, v)]
                            sem = esem[(name, ep)]
                            key = (k, name, ep)
                        else:
                            sem, val = dsem[name], v
                            key = (k, name)
                        if seen.get(key, 0) >= val:
                            continue
                        seen[key] = val
                        engine.wait_ge(sem, val)
                        self.stats[e][1] += 1
                    if kind == "op":
                        ins = fn(engine)
                        self.stats[e][0] += 1
                        if aux in needed[e]:
                            ins.then_inc(esem[(e, val_of[(e, aux)][0])], 1)
                    elif kind == "dma":
                        ins = fn(engine)
                        self.stats[e][0] += 1
                        ins.then_inc(dsem[aux], 16)

            getattr(block, engobj[e])(body)


class Arena:
    def __init__(self, ap, cap):
        self.ap, self.cap, self.off = ap, cap, 0

    def reset(self):
        self.off = 0

    def _take(self, n32):
        a = self.ap[:, self.off:self.off + n32]
        self.off += n32
        assert self.off <= self.cap, ("arena overflow", self.off, self.cap)
        return a

    @staticmethod
    def _shape(a, shape):
        if len(shape) == 1:
            return a
        if len(shape) == 2:
            return a.rearrange("p (a b) -> p a b", b=shape[1])
        if len(shape) == 3:
            return a.rearrange("p (a b c) -> p a b c", b=shape[1], c=shape[2])
        raise ValueError

    def f32(self, *shape):
        n = int(np.prod(shape))
        return self._shape(self._take(n), shape)

    def bf16(self, *shape):
        n = int(np.prod(shape))
        a = self._take((n + 1) // 2).bitcast(BF16)[:, 0:n]
        return self._shape(a, shape)


class Rot:
    def __init__(self, name, aps):
        self.name, self.aps, self.i = name, aps, -1

    def next(self):
        self.i = (self.i + 1) % len(self.aps)
        return self.aps[self.i], (self.name, self.i)


def make_blocks(T, nb):
    out, t = [], 0
    while t < T:
        n = min(nb, T - t)
        out.append((t, n))
        t += n
    return out


def build(T, depth, taps=(), phases=None, dbg=None, nreal=None):
    assert T % 128 == 0
    nreal = T if nreal is None else nreal
    NT = T // 128
    NCH = T // 64
    nc = bass.Bass("TRN2", target_bir_lowering=False)
    S = Sched(nc)
    allph = phases is None

    def dram(name, shape, dt, kind="Internal"):
        if name in taps:
            kind = "ExternalOutput"
        return nc.dram_tensor(name, shape, dt, kind=kind).ap()

    xT = dram("xT", [D, T], F32, "ExternalInput")
    cols_d = dram("cols", [depth, 128, NCOL], F32, "ExternalInput")
    w_in_d = dram("w_in", [depth, D, CIN], F32, "ExternalInput")
    wlora_d = dram("wlora", [depth, 128, 512], F32, "ExternalInput")
    glora_d = dram("glora", [depth, 128, 512], F32, "ExternalInput")
    w_orw_d = dram("w_out_rw", [depth, 512, D], F32, "ExternalInput")
    w_ofx_d = dram("w_out_fx", [depth, 512, D], F32, "ExternalInput")
    w_o_d = dram("w_o", [depth, D, D], F32, "ExternalInput")
    w_up_d = dram("w_up", [depth, D, DFF], F32, "ExternalInput")
    w_dn_d = dram("w_down", [depth, DFF, D], F32, "ExternalInput")
    outT = dram("outT", [D, T], F32, "ExternalOutput")

    hT = dram("hT", [D, T], F32)
    fT = dram("fT", [1792, T], F32)
    vrw_tm = dram("vrw_tm", [T, 512], BF16)
    qT = dram("qT", [512, T], BF16)
    kT = dram("kT", [512, T], BF16)
    vfx_tm = dram("vfx_tm", [T, 512], BF16)
    gT = dram("gT", [8, T], F32)
    gaug = dram("gaug", [8, 3, T], BF16)
    gatesT = dram("gatesT", [2048, T], BF16)
    yrwT = dram("yrwT", [512, T], BF16)
    yfxT = dram("yfxT", [512, T], BF16)
    arT = dram("arT", [512, NCH, 128], BF16)
    bkT = dram("bkT", [512, NCH, 128], BF16)
    bkh = dram("bkh", [NCH, 128, 512], BF16)
    wcD = dram("wcD", [512, NCH], F32)
    grwT = dram("grwT", [512, T], BF16)
    bonT = dram("bonT", [512, T], F32)
    yraw = dram("yraw", [512, T], F32)

    with ExitStack() as st:
        CAP = 49152
        arena_t = st.enter_context(nc.sbuf_tensor("arena", [128, CAP], F32))
        A = Arena(arena_t[:, :], CAP - 1536)
        ctail = arena_t[:, CAP - 1536:CAP]
        ident = ctail[:, 0:64].bitcast(BF16)
        ones_bf = ctail[:, 64:128].bitcast(BF16)
        blk64 = ctail[:, 128:192].bitcast(BF16)
        blk64m = ctail[:, 192:256].bitcast(BF16)
        maskneg = ctail[:, 256:320].bitcast(BF16)
        mask_rw = ctail[:, 320:384].bitcast(BF16)
        mask_lt = ctail[:, 384:416].bitcast(BF16)
        ones64 = ctail[:, 416:480]
        ones_row = ctail[:, 480:544]
        colsS = ctail[:, 544:544 + NCOL]
        colsX = ctail[:, 640:704]
        identf = ctail[:, 704:712]
        gtm_all = ctail[:, 720:720 + 8 * NT].rearrange("p (h t) -> p h t", h=8)
        ps = [st.enter_context(nc.psum_tensor("ps%d" % i, [128, 512], F32)) for i in range(8)]
        psb = [p[:, :].bitcast(BF16) for p in ps]

        def mm(out, lhsT, rhs, s0, s1, rd, wr):
            S.op("pe", lambda e: e.matmul(out, lhsT=lhsT, rhs=rhs, start=s0, stop=s1), rd, wr)

        def tr(out, in_, rd, wr):
            S.op("pe", lambda e: e.transpose(out, in_, ident[0:in_.shape[0], 0:in_.shape[0]]), rd, wr)

        def act(out, in_, func, rd, wr, bias=0.0, scale=1.0):
            S.op("act", lambda e: e.activation(out=out, in_=in_, func=func, bias=bias, scale=scale), rd, wr)

        def tt(eng, out, in0, in1, op, rd, wr):
            S.op(eng, lambda e: e.tensor_tensor(out=out, in0=in0, in1=in1, op=op), rd, wr)

        def ts(eng, out, in0, s1, s2, op0, op1, rd, wr):
            if s2 is None:
                S.op(eng, lambda e: e.tensor_scalar(out=out, in0=in0, scalar1=s1, scalar2=None, op0=op0), rd, wr)
            else:
                S.op(eng, lambda e: e.tensor_scalar(out=out, in0=in0, scalar1=s1, scalar2=s2, op0=op0, op1=op1), rd, wr)

        def stt(out, in0, scalar, in1, op0, op1, rd, wr):
            S.op("dve", lambda e: e.scalar_tensor_tensor(out=out, in0=in0, scalar=scalar, in1=in1, op0=op0, op1=op1), rd, wr)

        def cp(eng, out, in_, rd, wr):
            if eng == "act":
                S.op("act", lambda e: e.activation(out=out, in_=in_, func=AF.Copy), rd, wr)
            else:
                S.op(eng, lambda e: e.tensor_copy(out=out, in_=in_), rd, wr)

        def mset(eng, ap, v, wr):
            S.op(eng, lambda e: e.memset(ap, v), (), wr)

        def ld(q, out, in_, key, rd=(), slow=False):
            S.dma(q, lambda e: e.dma_start(out=out, in_=in_, allow_slow_non_contiguous=slow), ("L",) + tuple(key),
                  reads=rd, writes=[key])

        def sto(q, out, in_, key, dkeys=(), slow=False):
            S.dma(q, lambda e: e.dma_start(out=out, in_=in_, allow_slow_non_contiguous=slow), ("S",) + tuple(key),
                  reads=[key], writes=list(dkeys))

        def asel(out, in_, pattern, cmp_, fill, base, cm, rd, wr):
            S.op("pool", lambda e: e.affine_select(out=out, in_=in_, pattern=pattern, compare_op=cmp_, fill=fill,
                                                   base=base, channel_multiplier=cm), rd, wr)

        K = "const"
        mset("pool", ident, 1.0, [K])
        asel(ident, ident, [[-1, 128]], ALU.is_equal, 0.0, 0, 1, [K], [K])
        mset("pool", ones_bf, 1.0, [K])
        mset("pool", blk64, 0.0, [K])
        mset("pool", blk64[0:64, 0:64], 1.0, [K])
        mset("pool", blk64[64:128, 64:128], 1.0, [K])
        mset("pool", blk64m, 0.0, [K])
        mset("pool", blk64m[0:64, 0:64], 1.0 / 64, [K])
        mset("pool", blk64m[64:128, 64:128], 1.0 / 64, [K])
        mset("pool", maskneg, 0.0, [K])
        asel(maskneg, maskneg, [[1, 128]], ALU.is_ge, -30000.0, 0, -1, [K], [K])
        mset("pool", mask_rw, 1.0, [K])
        asel(mask_rw[0:64, 0:64], mask_rw[0:64, 0:64], [[1, 64]], ALU.is_gt, 0.0, 0, -1, [K], [K])
        asel(mask_rw[0:64, 64:128], mask_rw[0:64, 64:128], [[1, 64]], ALU.is_ge, 0.0, 0, -1, [K], [K])
        asel(mask_rw[64:128, 0:64], mask_rw[64:128, 0:64], [[1, 64]], ALU.is_gt, 0.0, 0, -1, [K], [K])
        asel(mask_rw[64:128, 64:128], mask_rw[64:128, 64:128], [[1, 64]], ALU.is_ge, 0.0, 0, -1, [K], [K])
        mset("pool", mask_lt, 1.0, [K])
        asel(mask_lt[0:64, :], mask_lt[0:64, :], [[-1, 64]], ALU.is_gt, 0.0, 0, 1, [K], [K])
        mset("pool", ones64, 1.0, [K])
        mset("pool", identf[0:8, :], 1.0, [K])
        asel(identf[0:8, :], identf[0:8, :], [[-1, 8]], ALU.is_equal, 0.0, 0, 1, [K], [K])
        mset("pool", ones_row, 1.0, [K])
        S.barrier()

        def load_w_bf16(wd2, dest, kchunks, ncols, stage_rot, piece):
            i = 0
            for kc in range(kchunks):
                c0 = 0
                while c0 < ncols:
                    w = min(piece, ncols - c0)
                    stg, key = stage_rot.next()
                    ld("sp" if i % 2 == 0 else "act", stg[:, 0:w], wd2[kc * 128:(kc + 1) * 128, c0:c0 + w], key)
                    cp("pool", dest[:, kc, c0:c0 + w], stg[:, 0:w], [key], ["W"])
                    c0 += w
                    i += 1

        def load_cols(l):
            ld("sp", colsS, cols_d[l], ("cols",))
            ts("dve", colsX[:, 0:1], colsS[:, C_QG:C_QG + 1], 0.125, None, ALU.mult, None, [("cols",)], [("colsx",)])
            ts("dve", colsX[:, 1:5], colsS[:, C_KA:C_KA + 4], -1.0, 1.0, ALU.mult, ALU.add, [("cols",)], [("colsx",)])
            ts("dve", colsX[:, 8:9], colsS[:, C_BF:C_BF + 1], -1.0, None, ALU.mult, None, [("cols",)], [("colsx",)])

        CK = [("cols",), ("colsx",)]

        def phase1(l, hsrc):
            A.reset()
            NB = 512
            Wb = A.bf16(8, CIN)
            stage = Rot("stg", [A.f32(673) for _ in range(2)])
            hblk = Rot("hblk", [A.f32(8, NB) for _ in range(2)])
            sq = A.bf16(8, NB)
            uT = A.bf16(8, NB)
            lnb = A.f32(NB)
            rstd = A.f32(NB)
            fbuf = Rot("fbuf", [A.f32(NB + 1) for _ in range(2)])
            dtmp = Rot("dtmp", [A.f32(NB) for _ in range(2)])
            of32 = Rot("of32", [A.f32(NB) for _ in range(4)])
            obf = Rot("obf", [A.bf16(NB) for _ in range(4)])
            sqh = Rot("sqh", [A.bf16(NB) for _ in range(2)])
            rsh = Rot("rsh", [A.f32(NB) for _ in range(2)])
            vbf = A.bf16(4, NB)
            vtm = Rot("vtm", [A.bf16(512) for _ in range(3)])
            carry = A.f32(16)
            gsm = A.f32(4, NB)
            gsb = A.bf16(3, NB)
            gcar = A.f32(2)
            load_cols(l)
            load_w_bf16(w_in_d[l], Wb, 8, CIN, stage, 673)
            mset("dve", carry, 0.0, ["carry"])
            mset("dve", gcar, 0.0, ["gcar"])
            hv = hsrc.rearrange("(c p) t -> p c t", p=128)
            pmain = Rot("pm", [ps[0], ps[1], ps[2], ps[3]])
            pstat = Rot("pst", [ps[4], ps[5]])
            ptrn = Rot("ptr", [psb[6], psb[7]])
            for bi, (t0, n) in enumerate(make_blocks(T, NB)):
                hb, hk = hblk.next()
                ld("sp", hb[:, :, 0:n], hv[:, :, t0:t0 + n], hk)
                act(sq[:, :, 0:n], hb[:, :, 0:n], AF.Square, [hk], ["sq"])
                pss, pk = pstat.next()
                for c in range(8):
                    mm(pss[:, 0:n], ones_bf, sq[:, c, 0:n], c == 0, c == 7, ["sq", K], [pk])
                act(lnb[:, 0:n], pss[:, 0:n], AF.Ln, [pk], ["lnb"], bias=NORM_EPS, scale=1.0 / D)
                act(rstd[:, 0:n], lnb[:, 0:n], AF.Exp, ["lnb"], ["rstd"], scale=-0.5)
                for c in range(8):
                    stt(uT[:, c, 0:n], hb[:, c, 0:n], colsS[:, C_NMIX + c:C_NMIX + c + 1], rstd[:, 0:n],
                        ALU.mult, ALU.mult, [hk, "rstd"] + CK, ["uT"])
                chunks = [(i, i * 128, 128) for i in range(14)] + [(14 + i, 1792 + i * 128, 128) for i in range(8)] \
                    + [(26, 3328, 8)] + [(27 + i, 3336 + i * 128, 128) for i in range(16)]
                for fc, cs_, m in chunks:
                    pm, pmk = pmain.next()
                    for kc in range(8):
                        mm(pm[0:m, 0:n], Wb[:, kc, cs_:cs_ + m], uT[:, kc, 0:n], kc == 0, kc == 7, ["W", "uT"], [pmk])
                    if fc < 14:
                        fb, fk = fbuf.next()
                        cp("dve", fb[:, 0:1], carry[:, fc:fc + 1], ["carry"], [fk])
                        cp("act", fb[:, 1:n + 1], pm[:, 0:n], [pmk], [fk])
                        cp("pool", carry[:, fc:fc + 1], fb[:, n:n + 1], [fk], ["carry"])
                        dt_, dk = dtmp.next()
                        tt("dve", dt_[:, 0:n], fb[:, 0:n], fb[:, 1:n + 1], ALU.subtract, [fk], [dk])
                        o, ok = of32.next()
                        stt(o[:, 0:n], dt_[:, 0:n], colsS[:, C_MU + fc:C_MU + fc + 1], fb[:, 1:n + 1],
                            ALU.mult, ALU.add, [dk, fk] + CK, [ok])
                        if 8 <= fc < 12:
                            cp("pool", vbf[:, fc - 8, 0:n], o[:, 0:n], [ok], ["vbf"])
                        sto("pool", fT[fc * 128:(fc + 1) * 128, t0:t0 + n], o[:, 0:n], ok)
                    elif fc < 22:
                        isq = fc < 18
                        s_, sk = sqh.next()
                        act(s_[:, 0:n], pm[:, 0:n], AF.Square, [pmk], [sk])
                        pss, pk = pstat.next()
                        mm(pss[:, 0:n], blk64m, s_[:, 0:n], True, True, [sk, K], [pk])
                        r_, rk = rsh.next()
                        act(r_[:, 0:n], pss[:, 0:n], AF.Ln, [pk], [rk], bias=NORM_EPS)
                        act(r_[:, 0:n], r_[:, 0:n], AF.Exp, [rk], [rk], scale=-0.5)
                        gcol = colsX[:, 0:1] if isq else colsS[:, C_KG:C_KG + 1]
                        o, ok = obf.next()
                        stt(o[:, 0:n], pm[:, 0:n], gcol, r_[:, 0:n], ALU.mult, ALU.mult, [pmk, rk] + CK, [ok])
                        dst = qT if isq else kT
                        r0 = (fc - 14) * 128 if isq else (fc - 18) * 128
                        sto("pool", dst[r0:r0 + 128, t0:t0 + n], o[:, 0:n], ok)
                    elif fc == 26:
                        act(gsm[0:8, 0, 0:n], pm[0:8, 0:n], AF.Exp, [pmk] + CK, ["gsm0"], bias=colsX[0:8, 8:9], scale=-1.0)
                        act(gsm[0:8, 0, 0:n], gsm[0:8, 0, 0:n], AF.Ln, ["gsm0"], ["gsm0"], bias=1.0)
                        S.op("dve", lambda e, n=n: e.tensor_tensor_scan(
                            out=gsm[0:8, 1, 0:n], data0=ones_row[0:8, 0:1].to_broadcast([8, n]), data1=gsm[0:8, 0, 0:n],
                            initial=gcar[0:8, 0:1], op0=ALU.mult, op1=ALU.add), ["gsm0", "gcar", K], ["gsm1"])
                        cp("dve", gcar[0:8, 0:1], gsm[0:8, 1, n - 1:n], ["gsm1"], ["gcar"])
                        sto("sp", gT[:, t0:t0 + n], gsm[0:8, 1, 0:n], "gsm1", [("G",)])
                        pg_, pgk_ = pstat.next()
                        ntl = n // 128
                        for j in range(ntl):
                            S.op("pe", lambda e, j=j, pg_=pg_: e.transpose(pg_[:, j * 8:(j + 1) * 8], gsm[0:8, 1, j * 128:(j + 1) * 128],
                                                                    identf[0:8, 0:8]), ["gsm1", K], [pgk_])
                        cp("dve", gtm_all[:, :, t0 // 128:t0 // 128 + ntl],
                           pg_[:, 0:ntl * 8].rearrange("p (t h) -> p h t", h=8), [pgk_], ["gtm_all"])
                        ts("dve", gsb[0:8, 0, 0:n], gsm[0:8, 1, 0:n], -1.0, None, ALU.mult, None, ["gsm1"], ["gsb0"])
                        stt(gsm[0:8, 2, 0:n], gsm[0:8, 1, 0:n], -1.0, gsb[0:8, 0, 0:n], ALU.mult, ALU.subtract,
                            ["gsm1", "gsb0"], ["gsm2"])
                        cp("dve", gsb[0:8, 1, 0:n], gsm[0:8, 2, 0:n], ["gsm2"], ["gsb1"])
                        tt("dve", gsm[0:8, 3, 0:n], gsm[0:8, 2, 0:n], gsb[0:8, 1, 0:n], ALU.subtract, ["gsm2", "gsb1"], ["gsm3"])
                        cp("dve", gsb[0:8, 2, 0:n], gsm[0:8, 3, 0:n], ["gsm3"], ["gsb2"])
                        for j in range(3):
                            sto("sp", gaug[:, j, t0:t0 + n], gsb[0:8, j, 0:n], "gsb%d" % j, [("G",)])
                    else:
                        gc = fc - 27
                        o, ok = obf.next()
                        act(o[:, 0:n], pm[:, 0:n], AF.Sigmoid, [pmk] + CK, [ok], bias=colsS[:, C_BG + gc:C_BG + gc + 1])
                        sto("pool", gatesT[gc * 128:(gc + 1) * 128, t0:t0 + n], o[:, 0:n], ok)
                for tt_ in range(n // 128):
                    pm, pmk = pmain.next()
                    for kc in range(8):
                        mm(pm[:, 0:512], uT[:, kc, tt_ * 128:(tt_ + 1) * 128], Wb[:, kc, 2816:3328], kc == 0, kc == 7,
                           ["W", "uT"], [pmk])
                    v_, vk = vtm.next()
                    cp("act", v_[:, 0:512], pm[:, 0:512], [pmk], [vk])
                    sto("sp", vfx_tm[t0 + tt_ * 128:t0 + (tt_ + 1) * 128, :], v_[:, 0:512], vk)
                    pt, ptk = ptrn.next()
                    for c in range(4):
                        tr(pt[:, c * 128:(c + 1) * 128], vbf[:, c, tt_ * 128:(tt_ + 1) * 128], ["vbf", K], [ptk])
                    v_, vk = vtm.next()
                    cp("dve", v_[:, 0:512], pt[:, 0:512], [ptk], [vk])
                    sto("sp", vrw_tm[t0 + tt_ * 128:t0 + (tt_ + 1) * 128, :], v_[:, 0:512], vk)
            if "gtmD" in taps:
                gd_ = dram("gtmD", [128, 8 * NT], F32)
                sto("sp", gd_[:, :], ctail[:, 720:720 + 8 * NT], "gtm_all")
            S.barrier()

        def phase2(l):
            A.reset()
            NB = 512
            wl = A.bf16(512)
            gl = A.bf16(512)
            stage = Rot("stg", [A.f32(512) for _ in range(2)])
            x12 = A.f32(NB)
            x13 = A.f32(NB)
            twa = A.bf16(NB)
            sg = A.bf16(NB)
            rkv = Rot("rkv", [A.f32(3, NB) for _ in range(2)])
            sigw = A.f32(NB)
            aa = A.f32(NB)
            gob = Rot("gob", [A.bf16(NB) for _ in range(2)])
            sqk = A.bf16(NB)
            rn = A.f32(NB)
            kkn = A.f32(NB)
            cs = A.f32(NB)
            ep = A.f32(NB)
            em = A.f32(NB)
            epv = A.f32(NB)
            d2 = A.f32(NB)
            eh = A.f32(NB)
            bb = A.f32(NB)
            t1 = A.f32(NB)
            kmod = A.f32(NB)
            ARo = Rot("ARo", [A.bf16(8, 128) for _ in range(2)])
            BKo = Rot("BKo", [A.bf16(8, 128) for _ in range(2)])
            BHo = A.bf16(8, 128)
            BHt = Rot("BHt", [A.bf16(8, 128) for _ in range(2)])
            wco = Rot("wco", [A.f32(8) for _ in range(2)])
            rkb = A.bf16(NB)
            bon = Rot("bon", [A.f32(NB) for _ in range(2)])
            load_cols(l)
            for i, (src, dst) in enumerate(((wlora_d[l], wl), (glora_d[l], gl))):
                stg, key = stage.next()
                ld("sp", stg[:, 0:512], src, key)
                cp("pool", dst[:, 0:512], stg[:, 0:512], [key], ["W"])
            pA = Rot("pA", [ps[0], ps[1], ps[2], ps[3], ps[4], ps[5]])
            ptrn = Rot("ptr", [psb[6], psb[7]])
            for bi, (t0, n) in enumerate(make_blocks(T, NB)):
                nch = n // 64
                ch0 = t0 // 64
                ld("sp", x12[:, 0:n], fT[1536:1664, t0:t0 + n], ("x12",))
                ld("sp", x13[:, 0:n], fT[1664:1792, t0:t0 + n], ("x13",))
                act(twa[0:64, 0:n], x12[0:64, 0:n], AF.Tanh, [("x12",)], ["twa"])
                cp("act", twa[64:128, 0:n], x12[64:128, 0:n], [("x12",)], ["twb"])
                act(sg[:, 0:n], x13[:, 0:n], AF.Sigmoid, [("x13",)], ["sg"])
                for fc in range(4):
                    rk_, rkk = rkv.next()
                    for j in range(3):
                        ld("sp", rk_[:, j, 0:n], fT[j * 512 + fc * 128:j * 512 + (fc + 1) * 128, t0:t0 + n], rkk)
                    r_, k_, v_ = rk_[:, 0, 0:n], rk_[:, 1, 0:n], rk_[:, 2, 0:n]
                    pw, pwk = pA.next()
                    mm(pw[:, 0:n], wl[0:64, fc * 128:(fc + 1) * 128], twa[0:64, 0:n], True, True, ["W", "twa"], [pwk])
                    act(sigw[:, 0:n], pw[:, 0:n], AF.Sigmoid, [pwk] + CK, ["sigw"], bias=colsS[:, C_W0 + fc:C_W0 + fc + 1])
                    pa, pak = pA.next()
                    mm(pa[:, 0:n], wl[64:128, fc * 128:(fc + 1) * 128], twa[64:128, 0:n], True, True, ["W", "twb"], [pak])
                    act(aa[:, 0:n], pa[:, 0:n], AF.Sigmoid, [pak] + CK, ["aa"], bias=colsS[:, C_A0 + fc:C_A0 + fc + 1])
                    pg, pgk = pA.next()
                    mm(pg[:, 0:n], gl[:, fc * 128:(fc + 1) * 128], sg[:, 0:n], True, True, ["W", "sg"], [pgk])
                    go, gok = gob.next()
                    cp("act", go[:, 0:n], pg[:, 0:n], [pgk], [gok])
                    sto("pool", grwT[fc * 128:(fc + 1) * 128, t0:t0 + n], go[:, 0:n], gok)
                    kkc = colsS[:, C_KK + fc:C_KK + fc + 1]
                    act(sqk[:, 0:n], k_, AF.Square, [rkk] + CK, ["sqk"], scale=kkc)
                    pn, pnk = pA.next()
                    mm(pn[:, 0:n], blk64, sqk[:, 0:n], True, True, ["sqk", K], [pnk])
                    act(rn[:, 0:n], pn[:, 0:n], AF.Ln, [pnk], ["rn"], bias=1e-18)
                    act(rn[:, 0:n], rn[:, 0:n], AF.Exp, ["rn"], ["rn"], scale=-0.5)
                    stt(kkn[:, 0:n], k_, kkc, rn[:, 0:n], ALU.mult, ALU.mult, [rkk, "rn"] + CK, ["kkn"])
                    for c in range(nch):
                        S.op("dve", lambda e, c=c: e.tensor_tensor_scan(
                            out=cs[:, c * 64:(c + 1) * 64], data0=ones64[:, 0:64], data1=sigw[:, c * 64:(c + 1) * 64],
                            initial=0.0, op0=ALU.mult, op1=ALU.add), ["sigw", K], ["cs"])
                    act(ep[:, 0:n], cs[:, 0:n], AF.Exp, ["cs"], ["ep"], scale=-LWS)
                    act(em[:, 0:n], cs[:, 0:n], AF.Exp, ["cs"], ["em"], scale=LWS)
                    ep3 = ep[:, 0:n].rearrange("p (c s) -> p c s", s=64)
                    epv3 = epv[:, 0:n].rearrange("p (c s) -> p c s", s=64)
                    cs3 = cs[:, 0:n].rearrange("p (c s) -> p c s", s=64)
                    cp("pool", epv3[:, :, 1:64], ep3[:, :, 0:63], ["ep"], ["epv"])
                    mset("pool", epv3[:, :, 0:1], 1.0, ["epv"])
                    tt("dve", d2[:, 0:n].rearrange("p (c s) -> p c s", s=64), cs3[:, :, 63:64].to_broadcast([128, nch, 64]),
                       cs3, ALU.subtract, ["cs"], ["d2"])
                    act(eh[:, 0:n], d2[:, 0:n], AF.Exp, ["d2"], ["eh"], scale=-LWS)
                    wo, wok = wco.next()
                    cp("pool", wo[:, 0:nch], ep3[:, :, 63], ["ep"], [wok])
                    sto("pool", wcD[fc * 128:(fc + 1) * 128, ch0:ch0 + nch], wo[:, 0:nch], wok)
                    aro, ark = ARo.next()
                    bko, bkk = BKo.next()
                    v3 = lambda ap: ap.rearrange("p (c s) -> p c s", s=64)
                    tt("dve", aro[:, 0:nch, 64:128], v3(r_), ep3, ALU.mult, [rkk, "ep"], [ark])
                    stt(aro[:, 0:nch, 0:64], v3(kkn[:, 0:n]), -1.0, epv3, ALU.mult, ALU.mult, ["kkn", "epv"], [ark])
                    tt("pool", bb[:, 0:n], kkn[:, 0:n], aa[:, 0:n], ALU.mult, ["kkn", "aa"], ["bb"])
                    tt("dve", bko[:, 0:nch, 0:64], v3(bb[:, 0:n]), v3(em[:, 0:n]), ALU.mult, ["bb", "em"], [bkk])
                    tt("pool", BHo[:, 0:nch, 0:64], v3(bb[:, 0:n]), v3(eh[:, 0:n]), ALU.mult, ["bb", "eh"], ["BHo"])
                    ts("dve", t1[:, 0:n], aa[:, 0:n], colsS[:, C_KA + fc:C_KA + fc + 1], colsX[:, 1 + fc:2 + fc],
                       ALU.mult, ALU.add, ["aa"] + CK, ["t1"])
                    tt("dve", kmod[:, 0:n], k_, t1[:, 0:n], ALU.mult, [rkk, "t1"], ["kmod"])
                    tt("dve", bko[:, 0:nch, 64:128], v3(kmod[:, 0:n]), v3(em[:, 0:n]), ALU.mult, ["kmod", "em"], [bkk])
                    tt("pool", BHo[:, 0:nch, 64:128], v3(kmod[:, 0:n]), v3(eh[:, 0:n]), ALU.mult, ["kmod", "eh"], ["BHo"])
                    sto("sp", arT[fc * 128:(fc + 1) * 128, ch0:ch0 + nch, :], aro[:, 0:nch, :], ark)
                    sto("sp", bkT[fc * 128:(fc + 1) * 128, ch0:ch0 + nch, :], bko[:, 0:nch, :], bkk)
                    pt, ptk = ptrn.next()
                    for c in range(nch):
                        tr(pt[:, c * 128:(c + 1) * 128], BHo[:, c, :], ["BHo", K], [ptk])
                    bt, btk = BHt.next()
                    cp("act", bt[:, 0:nch, :], pt[:, 0:nch * 128].rearrange("p (c s) -> p c s", s=128), [ptk], [btk])
                    sto("pool", bkh[ch0:ch0 + nch, :, fc * 128:(fc + 1) * 128].rearrange("c p f -> p c f"),
                        bt[:, 0:nch, :], btk)
                    stt(rkb[:, 0:n], r_, colsS[:, C_RK + fc:C_RK + fc + 1], kmod[:, 0:n], ALU.mult, ALU.mult,
                        [rkk, "kmod"] + CK, ["rkb"])
                    pb, pbk = pA.next()
                    mm(pb[:, 0:n], blk64, rkb[:, 0:n], True, True, ["rkb", K], [pbk])
                    bo, bok = bon.next()
                    tt("dve", bo[:, 0:n], pb[:, 0:n], v_, ALU.mult, [pbk, rkk], [bok])
                    sto("pool", bonT[fc * 128:(fc + 1) * 128, t0:t0 + n], bo[:, 0:n], bok)
            S.barrier()

        def phase3(l):
            A.reset()
            NBC = 8
            AR = Rot("AR", [A.bf16(8, NBC, 128) for _ in range(2)])
            BK = Rot("BK", [A.bf16(8, NBC, 128) for _ in range(2)])
            AAt = Rot("AA", [A.bf16(NBC, 8, 64) for _ in range(2)])
            HV = Rot("HV", [A.bf16(NBC + 1, 8, 64) for _ in range(2)])
            UV = Rot("UV", [A.bf16(NBC, 8, 64) for _ in range(2)])
            BH = Rot("BH", [A.bf16(NBC, 8, 64) for _ in range(2)])
            WC = Rot("WC", [A.f32(8, NBC) for _ in range(2)])
            ATr = Rot("ATr", [A.bf16(8, 64) for _ in range(2)])
            Mx = Rot("Mx", [A.bf16(8, 64) for _ in range(4)])
            Mt = Rot("Mt", [A.bf16(8, 64) for _ in range(4)])
            Pp = Rot("Pp", [A.bf16(8, 64) for _ in range(3)])
            Zb = A.bf16(8, 64)
            H32 = A.f32(8, 64)
            Htmp = A.f32(8, 64)
            ybuf = Rot("yb", [A.f32(8, 512) for _ in range(2)])
            identH = A.bf16(8, 64)
            for h in range(8):
                cp("pool", identH[0:64, h, :], ident[0:64, 0:64], [K], ["identH"])
            mset("dve", H32[0:64], 0.0, ["H32"])
            pAT = [ps[0], ps[1]]
            pinv = Rot("pinv", [ps[2], ps[3], ps[4]])
            pZU, pH, pY = ps[5], ps[6], ps[7]
            arv = arT.rearrange("(h j) c x -> j h c x", j=64)
            bkv = bkT.rearrange("(h j) c x -> j h c x", j=64)
            wcv = wcD.rearrange("(h j) c -> j h c", j=64)
            hv_prev = None
            for bi, (t0, n) in enumerate(make_blocks(T, NBC * 64)):
                nch = n // 64
                ch0 = t0 // 64
                ar, ark = AR.next()
                bk, bkk = BK.next()
                aat, aak = AAt.next()
                hvb, hvk = HV.next()
                uv, uvk = UV.next()
                bh, bhk = BH.next()
                wc, wck = WC.next()
                yb, ybk = ybuf.next()
                for h in range(8):
                    q = "sp" if h % 2 == 0 else "act"
                    ld(q, ar[0:64, h, 0:nch, :], arv[:, h, ch0:ch0 + nch, :], ark)
                    ld(q, bk[0:64, h, 0:nch, :], bkv[:, h, ch0:ch0 + nch, :], bkk)
                    ld(q, aat[0:64, 0:nch, h, :], arv[:, h, ch0:ch0 + nch, 0:64], (aak[0], aak[1], "a"))
                    ld(q, wc[0:64, h, 0:nch], wcv[:, h, ch0:ch0 + nch], wck, slow=True)
                vsrc = vrw_tm[t0:t0 + n, :].rearrange("(c s) (h i) -> s c h i", s=64, i=64)
                ld("sp", hvb[64:128, 0:nch, :, :], vsrc, (hvk[0], hvk[1], "v"))
                ld("act", uv[64:128, 0:nch, :, :], vsrc, (uvk[0], uvk[1], "v"))
                ld("sp", bh[:, 0:nch, :, :], bkh[ch0:ch0 + nch, :, :].rearrange("c p (h j) -> p c h j", j=64), bhk)
                cp("act", hvb[0:64, 0, :, :], H32[0:64], ["H32"], [(hvk[0], hvk[1], "h", 0)])
                for c in range(nch):
                    for h in range(8):
                        pt = pAT[h // 4]
                        mm(pt[:, (h % 4) * 128:(h % 4 + 1) * 128], bk[0:64, h, c, :], ar[0:64, h, c, :], True, True,
                           [ark, bkk], [("pAT", h // 4)])
                    atr, atrk = ATr.next()
                    mx, mxk = Mx.next()
                    for hh in range(2):
                        p4 = pAT[hh][:, :].rearrange("p (h x) -> p h x", x=128)
                        mk4 = mask_rw[:, :].rearrange("p (o x) -> p o x", o=1)
                        tt("dve", atr[:, hh * 4:(hh + 1) * 4, :], p4[:, :, 64:128], mk4[:, :, 64:128].to_broadcast([128, 4, 64]),
                           ALU.mult, [("pAT", hh), K], [atrk])
                        tt("dve", mx[0:64, hh * 4:(hh + 1) * 4, :], p4[0:64, :, 0:64], mk4[0:64, :, 0:64].to_broadcast([64, 4, 64]),
                           ALU.mult, [("pAT", hh), K], [mxk])
                        tt("dve", aat[64:128, c, hh * 4:(hh + 1) * 4, :], p4[64:128, :, 0:64],
                           mk4[64:128, :, 0:64].to_broadcast([64, 4, 64]), ALU.mult, [("pAT", hh), K], [(aak[0], aak[1], "k", c)])
                    pv, pvk = pinv.next()
                    for h in range(8):
                        mm(pv[0:64, h * 64:(h + 1) * 64], ar[0:64, h, c, 0:64], bk[0:64, h, c, 0:64], True, True,
                           [ark, bkk], [pvk])
                    mt, mtk = Mt.next()
                    tt("dve", mt[0:64], pv[0:64, :].rearrange("p (h x) -> p h x", x=64),
                       mask_lt[0:64, :].rearrange("p (o x) -> p o x", o=1).to_broadcast([64, 8, 64]), ALU.mult, [pvk, K], [mtk])
                    pp, ppk = Pp.next()
                    tt("pool", pp[0:64], mx[0:64], identH[0:64], ALU.add, [mxk, "identH"], [ppk])
                    X, Xk, Xt, Xtk = mx, mxk, mt, mtk
                    for lev in range(5):
                        last = lev == 4
                        pv, pvk = pinv.next()
                        for h in range(8):
                            mm(pv[0:64, h * 64:(h + 1) * 64], X[0:64, h, :], Xt[0:64, h, :], True, True, [Xk, Xtk], [pvk])
                        xt2, xt2k = Mt.next()
                        cp("act", xt2[0:64], pv[0:64, :].rearrange("p (h x) -> p h x", x=64), [pvk], [xt2k])
                        if not last:
                            pv2, pv2k = pinv.next()
                            for h in range(8):
                                mm(pv2[0:64, h * 64:(h + 1) * 64], Xt[0:64, h, :], X[0:64, h, :], True, True, [Xk, Xtk], [pv2k])
                            x2, x2k = Mx.next()
                            cp("dve", x2[0:64], pv2[0:64, :].rearrange("p (h x) -> p h x", x=64), [pv2k], [x2k])
                        pv3, pv3k = pinv.next()
                        for h in range(8):
                            mm(pv3[0:64, h * 64:(h + 1) * 64], xt2[0:64, h, :], pp[0:64, h, :], True, True, [xt2k, ppk], [pv3k])
                        pn, pnk = Pp.next()
                        tt("dve", pn[0:64], pv3[0:64, :].rearrange("p (h x) -> p h x", x=64), pp[0:64], ALU.add, [pv3k, ppk], [pnk])
                        pp, ppk = pn, pnk
                        if not last:
                            X, Xk, Xt, Xtk = x2, x2k, xt2, xt2k
                    hk_c = (hvk[0], hvk[1], "h", c)
                    vk_hv = (hvk[0], hvk[1], "v")
                    vk_uv = (uvk[0], uvk[1], "v")
                    for h in range(8):
                        mm(pZU[0:64, h * 64:(h + 1) * 64], aat[:, c, h, :], hvb[:, c, h, :], True, True,
                           [(aak[0], aak[1], "a"), (aak[0], aak[1], "k", c), hk_c, vk_hv], ["pZU"])
                    cp("act", Zb[0:64], pZU[0:64, :].rearrange("p (h x) -> p h x", x=64), ["pZU"], ["Zb"])
                    for h in range(8):
                        mm(pZU[0:64, h * 64:(h + 1) * 64], pp[0:64, h, :], Zb[0:64, h, :], True, True, [ppk, "Zb"], ["pZU"])
                    uk_c = (uvk[0], uvk[1], "u", c)
                    cp("act", uv[0:64, c, :, :], pZU[0:64, :].rearrange("p (h x) -> p h x", x=64), ["pZU"], [uk_c])
                    for h in range(8):
                        mm(pY[0:64, h * 64:(h + 1) * 64], uv[:, c, h, :], atr[:, h, :], True, False, [uk_c, vk_uv, atrk], ["pY"])
                        mm(pY[0:64, h * 64:(h + 1) * 64], hvb[0:64, c, h, :], ar[0:64, h, c, 64:128], False, True, [hk_c, ark], ["pY"])
                    cp("dve", yb[0:64, :, c * 64:(c + 1) * 64], pY[0:64, :].rearrange("p (h x) -> p h x", x=64), ["pY"], [ybk])
                    for h in range(8):
                        mm(pH[0:64, h * 64:(h + 1) * 64], bh[:, c, h, :], uv[:, c, h, :], True, True, [bhk, uk_c, vk_uv], ["pH"])
                    tt("pool", Htmp[0:64], H32[0:64], wc[0:64, :, c:c + 1].to_broadcast([64, 8, 64]), ALU.mult, ["H32", wck], ["Htmp"])
                    tt("dve", H32[0:64], pH[0:64, :].rearrange("p (h x) -> p h x", x=64), Htmp[0:64], ALU.add, ["pH", "Htmp"], ["H32"])
                    if c + 1 < nch:
                        cp("act", hvb[0:64, c + 1, :, :], H32[0:64], ["H32"], [(hvk[0], hvk[1], "h", c + 1)])
                sto("pool", yraw[:, t0:t0 + n].rearrange("(h i) t -> i h t", i=64), yb[0:64, :, 0:n], ybk)
            S.barrier()

        def phase3b(l):
            A.reset()
            NB = 512
            yin = Rot("yin", [A.f32(NB) for _ in range(2)])
            bin_ = Rot("bin", [A.f32(NB) for _ in range(2)])
            gin = Rot("gin", [A.bf16(NB) for _ in range(2)])
            ybf = A.bf16(NB)
            yc = A.f32(NB)
            sqc = A.bf16(NB)
            rs = A.f32(NB)
            o1 = A.f32(NB)
            oo = Rot("oo", [A.bf16(NB) for _ in range(2)])
            load_cols(l)
            pA = Rot("pA", [ps[0], ps[1], ps[2], ps[3]])
            for bi, (t0, n) in enumerate(make_blocks(T, NB)):
                for fc in range(4):
                    y_, yk = yin.next()
                    b_, bk_ = bin_.next()
                    g_, gk = gin.next()
                    rows = slice(fc * 128, (fc + 1) * 128)
                    ld("sp", y_[:, 0:n], yraw[rows, t0:t0 + n], yk)
                    ld("act", b_[:, 0:n], bonT[rows, t0:t0 + n], bk_)
                    ld("sp", g_[:, 0:n], grwT[rows, t0:t0 + n], gk)
                    cp("pool", ybf[:, 0:n], y_[:, 0:n], [yk], ["ybf"])
                    p1, p1k = pA.next()
                    mm(p1[:, 0:n], blk64m, ybf[:, 0:n], True, True, ["ybf", K], [p1k])
                    tt("dve", yc[:, 0:n], y_[:, 0:n], p1[:, 0:n], ALU.subtract, [yk, p1k], ["yc"])
                    act(sqc[:, 0:n], yc[:, 0:n], AF.Square, ["yc"], ["sqc"])
                    p2, p2k = pA.next()
                    mm(p2[:, 0:n], blk64m, sqc[:, 0:n], True, True, ["sqc", K], [p2k])
                    act(rs[:, 0:n], p2[:, 0:n], AF.Ln, [p2k], ["rs"], bias=GN_EPS)
                    act(rs[:, 0:n], rs[:, 0:n], AF.Exp, ["rs"], ["rs"], scale=-0.5)
                    tt("dve", o1[:, 0:n], yc[:, 0:n], rs[:, 0:n], ALU.mult, ["yc", "rs"], ["o1"])
                    ts("dve", o1[:, 0:n], o1[:, 0:n], colsS[:, C_LNG + fc:C_LNG + fc + 1], colsS[:, C_LNB + fc:C_LNB + fc + 1],
                       ALU.mult, ALU.add, ["o1"] + CK, ["o1"])
                    tt("pool", o1[:, 0:n], o1[:, 0:n], b_[:, 0:n], ALU.add, ["o1", bk_], ["o1"])
                    o, ok = oo.next()
                    tt("dve", o[:, 0:n], o1[:, 0:n], g_[:, 0:n], ALU.mult, ["o1", gk], [ok])
                    sto("pool", yrwT[rows, t0:t0 + n], o[:, 0:n], ok)
            S.barrier()

        def phaseF(l):
            A.reset()
            Qa = Rot("Qa", [A.bf16(T) for _ in range(2)])
            Ka = Rot("Ka", [A.bf16(T) for _ in range(2)])
            Va = Rot("Va", [A.bf16(NT, 66) for _ in range(2)])
            Gb = Rot("Gb", [A.f32(NT) for _ in range(2)])
            pT = Rot("pT", [A.bf16(512) for _ in range(4)])
            osb = Rot("osb", [A.f32(512) for _ in range(2)])
            rc = Rot("rc", [A.f32(512) for _ in range(2)])
            yo = Rot("yo", [A.bf16(512) for _ in range(2)])
            pS = Rot("pS", [ps[0], ps[1], ps[2], ps[3]])
            pO = Rot("pO", [ps[4], ps[5]])
            pB = Rot("pB", [ps[6], ps[7]])
            qblocks = make_blocks(T, 512)
            for i_ in range(2):
                mset("pool", Va.aps[i_][:, :, 64:66], 1.0, [("Va", i_, "o")])
                mset("pool", Ka.aps[i_][64:67, :], 1.0, [("Ka", i_, "o")])
            for h in range(8):
                qa, qak = Qa.next()
                ka, kak = Ka.next()
                va, vak = Va.next()
                gb, gbk = Gb.next()
                ld("sp", qa[0:64, :], qT[h * 64:(h + 1) * 64, :], qak)
                ld("sp", qa[64:67, :], gaug[h, :, :], (qak[0], qak[1], "g"))
                ld("act", ka[0:64, :], kT[h * 64:(h + 1) * 64, :], kak)
                vsrc_ = vfx_tm[:, h * 64:(h + 1) * 64].rearrange("(t p) d -> p t d", p=128)
                for j0 in range(0, NT, 8):
                    j1 = min(NT, j0 + 8)
                    ld("act" if (j0 // 8) % 2 else "sp", va[:, j0:j1, 0:64], vsrc_[:, j0:j1, :], vak)
                cp("pool", gb[:, :], gtm_all[:, h, :], ["gtm_all"], [gbk])
                qdeps = [qak, (qak[0], qak[1], "g")]
                kdeps = [kak, (kak[0], kak[1], "o")]
                vdeps = [vak, (vak[0], vak[1], "o")]
                for (t0, n) in qblocks:
                    po, pok = pO.next()
                    nkt = (t0 + n) // 128
                    for kt in range(nkt):
                        ks = kt * 128
                        c0 = max(0, ks - t0)
                        N = n - c0
                        diag = ks >= t0
                        p_, pk_ = pS.next()
                        mm(p_[:, 0:N], ka[0:67, ks:ks + 128], qa[0:67, t0 + c0:t0 + n], True, not diag, kdeps + qdeps, [pk_])
                        if diag:
                            mm(p_[:, 0:128], ident, maskneg, False, True, [K], [pk_])
                        pt_, ptk_ = pT.next()
                        act(pt_[:, 0:N], p_[:, 0:N], AF.Exp, [pk_, gbk], [ptk_], bias=gb[:, kt:kt + 1])
                        mm(po[0:65, c0:n], va[:, kt, 0:65], pt_[:, 0:N], kt == 0, kt == nkt - 1, vdeps + [ptk_], [pok])
                    r_, rk_ = rc.next()
                    o_, ok_ = osb.next()
                    cp("act", o_[0:65, 0:n], po[0:65, 0:n], [pok], [ok_])
                    S.op("dve", lambda e, r_=r_, o_=o_, n=n: e.reciprocal(out=r_[64:65, 0:n], in_=o_[64:65, 0:n]), [ok_], [rk_])
                    pb, pbk = pB.next()
                    mm(pb[0:64, 0:n], ones_row[64:65, 0:64], r_[64:65, 0:n], True, True, [rk_, K], [pbk])
                    y_, yk_ = yo.next()
                    tt("dve", y_[0:64, 0:n], o_[0:64, 0:n], pb[0:64, 0:n], ALU.mult, [ok_, pbk], [yk_])
                    sto("pool", yfxT[h * 64:(h + 1) * 64, t0:t0 + n], y_[0:64, 0:n], yk_)
            S.barrier()

        def phase4(l, hsrc, hdst):
            A.reset()
            NB = 512
            Wrw = A.bf16(4, D)
            Wfx = A.bf16(4, D)
            Wo = A.bf16(8, D)
            stage = Rot("stg", [A.f32(1024) for _ in range(2)])
            hblk = Rot("hblk", [A.f32(8, NB) for _ in range(2)])
            gts = Rot("gts", [A.bf16(16, NB) for _ in range(2)])
            yrw = Rot("yrw", [A.bf16(4, NB) for _ in range(2)])
            yfx = Rot("yfx", [A.bf16(4, NB) for _ in range(2)])
            m1 = Rot("m1", [A.f32(NB) for _ in range(2)])
            m2 = Rot("m2", [A.f32(NB) for _ in range(2)])
            mg = A.bf16(8, NB)
            load_w_bf16(w_orw_d[l], Wrw, 4, D, stage, 1024)
            load_w_bf16(w_ofx_d[l], Wfx, 4, D, stage, 1024)
            load_w_bf16(w_o_d[l], Wo, 8, D, stage, 1024)
            hv = hsrc.rearrange("(c p) t -> p c t", p=128)
            hd = hdst.rearrange("(c p) t -> p c t", p=128)
            pA = Rot("pA", [ps[0], ps[1], ps[2], ps[3]])
            pO = Rot("pO", [ps[4], ps[5], ps[6], ps[7]])
            for bi, (t0, n) in enumerate(make_blocks(T, NB)):
                hb, hk = hblk.next()
                ld("sp", hb[:, :, 0:n], hv[:, :, t0:t0 + n], hk)
                g_, gk = gts.next()
                ld("act", g_[:, :, 0:n], gatesT.rearrange("(c p) t -> p c t", p=128)[:, :, t0:t0 + n], gk)
                a_, ak = yrw.next()
                ld("sp", a_[:, :, 0:n], yrwT.rearrange("(c p) t -> p c t", p=128)[:, :, t0:t0 + n], ak)
                b_, bk_ = yfx.next()
                ld("act", b_[:, :, 0:n], yfxT.rearrange("(c p) t -> p c t", p=128)[:, :, t0:t0 + n], bk_)
                for oc in range(8):
                    p1, p1k = pA.next()
                    for kc in range(4):
                        mm(p1[:, 0:n], Wrw[:, kc, oc * 128:(oc + 1) * 128], a_[:, kc, 0:n], kc == 0, kc == 3, ["W", ak], [p1k])
                    p2, p2k = pA.next()
                    for kc in range(4):
                        mm(p2[:, 0:n], Wfx[:, kc, oc * 128:(oc + 1) * 128], b_[:, kc, 0:n], kc == 0, kc == 3, ["W", bk_], [p2k])
                    x1, x1k = m1.next()
                    x2, x2k = m2.next()
                    tt("dve", x1[:, 0:n], p1[:, 0:n], g_[:, oc, 0:n], ALU.mult, [p1k, gk], [x1k])
                    tt("dve", x2[:, 0:n], p2[:, 0:n], g_[:, 8 + oc, 0:n], ALU.mult, [p2k, gk], [x2k])
                    tt("pool", mg[:, oc, 0:n], x1[:, 0:n], x2[:, 0:n], ALU.add, [x1k, x2k], [("mg", oc)])
                for oc in range(8):
                    p3, p3k = pO.next()
                    for kc in range(8):
                        mm(p3[:, 0:n], Wo[:, kc, oc * 128:(oc + 1) * 128], mg[:, kc, 0:n], kc == 0, kc == 7,
                           ["W"] + [("mg", i) for i in range(8)], [p3k])
                    tt("dve", hb[:, oc, 0:n], p3[:, 0:n], hb[:, oc, 0:n], ALU.add, [p3k, hk], [hk])
                if t0 + n > nreal:
                    mset("dve", hb[:, :, max(0, nreal - t0):n], 0.0, [hk])
                sto("pool", hd[:, :, t0:t0 + n], hb[:, :, 0:n], hk, [("h", bi)])
            S.barrier()

        def phase5(l, hsrc, hdst):
            A.reset()
            NB = 256
            Wu = A.bf16(8, DFF)
            Wd = A.bf16(32, D)
            stage = Rot("stg", [A.f32(1024) for _ in range(2)])
            hblk = Rot("hblk", [A.f32(8, NB) for _ in range(2)])
            sq = A.bf16(8, NB)
            zT = A.bf16(8, NB)
            lnb = A.f32(NB)
            rstd = A.f32(NB)
            rl = Rot("rl", [A.f32(NB) for _ in range(2)])
            actb = A.bf16(32, NB)
            load_cols(l)
            load_w_bf16(w_up_d[l], Wu, 8, DFF, stage, 1024)
            load_w_bf16(w_dn_d[l], Wd, 32, D, stage, 1024)
            hv = hsrc.rearrange("(c p) t -> p c t", p=128)
            hd = hdst.rearrange("(c p) t -> p c t", p=128)
            pU = Rot("pU", [ps[0], ps[1], ps[2], ps[3]])
            pD = Rot("pD", [ps[4], ps[5], ps[6]])
            for bi, (t0, n) in enumerate(make_blocks(T, NB)):
                hb, hk = hblk.next()
                ld("sp", hb[:, :, 0:n], hv[:, :, t0:t0 + n], hk)
                act(sq[:, :, 0:n], hb[:, :, 0:n], AF.Square, [hk], ["sq"])
                for c in range(8):
                    mm(ps[7][:, 0:n], ones_bf, sq[:, c, 0:n], c == 0, c == 7, ["sq", K], ["pss"])
                act(lnb[:, 0:n], ps[7][:, 0:n], AF.Ln, ["pss"], ["lnb"], bias=NORM_EPS, scale=1.0 / D)
                act(rstd[:, 0:n], lnb[:, 0:n], AF.Exp, ["lnb"], ["rstd"], scale=-0.5)
                for c in range(8):
                    stt(zT[:, c, 0:n], hb[:, c, 0:n], colsS[:, C_NMLP + c:C_NMLP + c + 1], rstd[:, 0:n],
                        ALU.mult, ALU.mult, [hk, "rstd"] + CK, ["zT"])
                for fc in range(32):
                    pu, puk = pU.next()
                    for kc in range(8):
                        mm(pu[:, 0:n], Wu[:, kc, fc * 128:(fc + 1) * 128], zT[:, kc, 0:n], kc == 0, kc == 7, ["W", "zT"], [puk])
                    r_, rk_ = rl.next()
                    act(r_[:, 0:n], pu[:, 0:n], AF.Relu, [puk], [rk_])
                    eng = "dve" if fc % 2 == 0 else "pool"
                    tt(eng, actb[:, fc, 0:n], r_[:, 0:n], r_[:, 0:n], ALU.mult, [rk_], [("ab", fc)])
                for oc in range(8):
                    pd, pdk = pD.next()
                    for kc in range(32):
                        mm(pd[:, 0:n], Wd[:, kc, oc * 128:(oc + 1) * 128], actb[:, kc, 0:n], kc == 0, kc == 31,
                           ["W", ("ab", kc)], [pdk])
                    tt("dve", hb[:, oc, 0:n], pd[:, 0:n], hb[:, oc, 0:n], ALU.add, [pdk, hk], [hk])
                if t0 + n > nreal:
                    mset("dve", hb[:, :, max(0, nreal - t0):n], 0.0, [hk])
                sto("pool", hd[:, :, t0:t0 + n], hb[:, :, 0:n], hk, [("h", bi)])
            S.barrier()

        for l in range(depth):
            hsrc = xT if l == 0 else hT
            hfin = outT if l == depth - 1 else hT
            if allph or 1 in phases:
                phase1(l, hsrc)
            if allph or 2 in phases:
                phase2(l)
            if allph or 3 in phases:
                phase3(l)
            if allph or 35 in phases:
                phase3b(l)
            if allph or 6 in phases:
                phaseF(l)
            if allph or 4 in phases:
                phase4(l, hsrc, hT)
            if allph or 5 in phases:
                phase5(l, hT, hfin)
        S.barrier()
        S.emit(st)
    return nc, S


TPAD = 8320
FUSED = False


def kernel(x, meta, norm_mix, w_in, b_gate, b_f, tm_mu, w0, w_lora_up, a0, a_lora_up, g_lora_up, k_k, k_a, r_k,
           lnx_g, lnx_b, q_gain, k_gain, w_out_rw, w_out_fx, w_o, norm_mlp, w_up, w_down, _T=TPAD, _depth=DEPTH):
    f = lambda a: np.ascontiguousarray(np.asarray(a, dtype=np.float32))
    x = f(x)
    B = x.shape[0]
    L = _depth
    cols = np.zeros((L, 128, NCOL), np.float32)

    def put(c0, v):
        n = v.shape[1] // 128
        cols[:, :, c0:c0 + n] = v.reshape(L, n, 128).transpose(0, 2, 1)

    put(C_NMIX, f(norm_mix)[:L]); put(C_BG, f(b_gate)[:L]); put(C_MU, f(tm_mu)[:L]); put(C_W0, f(w0)[:L])
    put(C_A0, f(a0)[:L]); put(C_KK, f(k_k)[:L]); put(C_KA, f(k_a)[:L]); put(C_RK, f(r_k)[:L].reshape(L, 512))
    put(C_LNG, f(lnx_g)[:L]); put(C_LNB, f(lnx_b)[:L])
    put(C_QG, np.tile(f(q_gain)[:L], (1, 2))); put(C_KG, np.tile(f(k_gain)[:L], (1, 2)))
    put(C_NMLP, f(norm_mlp)[:L])
    cols[:, 0:8, C_BF] = f(b_f)[:L]
    wlora = np.ascontiguousarray(np.concatenate([f(w_lora_up)[:L], f(a_lora_up)[:L]], axis=1))
    T = _T
    nreal = x.shape[1] + NMETA
    xts = []
    for b in range(B):
        xt = np.zeros((D, T), np.float32)
        xt[:, 0:NMETA] = f(meta).T
        xt[:, NMETA:nreal] = x[b].T
        xts.append(xt)
    wts = {"w_in": f(w_in)[:L], "wlora": wlora, "glora": f(g_lora_up)[:L], "w_out_rw": f(w_out_rw)[:L],
           "w_out_fx": f(w_out_fx)[:L], "w_o": f(w_o)[:L], "w_up": f(w_up)[:L], "w_down": f(w_down)[:L], "cols": cols}
    if FUSED:
        nc, S = build(T, L, nreal=nreal)
        in_maps = [dict(wts, xT=xts[b]) for b in range(B)]
        res = run_bass_kernel_spmd(nc, in_maps, core_ids=list(range(B)))
        outs = [res.results[b]["outT"] for b in range(B)]
    else:
        nc, S = build(T, 1, nreal=nreal)
        outs = xts
        for l in range(L):
            wl_ = {k: np.ascontiguousarray(v[l:l + 1]) for k, v in wts.items()}
            in_maps = [dict(wl_, xT=np.ascontiguousarray(outs[b])) for b in range(B)]
            res = run_bass_kernel_spmd(nc, in_maps, core_ids=list(range(B)))
            outs = [res.results[b]["outT"] for b in range(B)]
    out = np.empty((B, x.shape[1], D), np.float32)
    for b in range(B):
        out[b] = outs[b][:, NMETA:nreal].T
    return out
```
